# Optimizing a Trainium2 kernel written in Bass

```python
import jax, jax.numpy as jnp
from jax import lax
import numpy as np

D_MODEL = 4096
BATCH = 2
SEQ = 8192
DEPTH = 2

GRID_W = 64
CTX_LEN = 256
A_HEADS = 16
A_HEAD_DIM = 128
D_A = A_HEADS * A_HEAD_DIM
D_B = D_MODEL // 2
CONV_K = 31
CONV_PAD = (CONV_K - 1) // 2
FFN_MULT = 256
D_FF = -(-8 * D_MODEL // (3 * FFN_MULT)) * FFN_MULT
CHUNK = 64
EPS = 1e-6
POS_BASE = 10000.0
MOD_DIM = 6 * D_MODEL
P_IN = 5 * D_A + 2 * D_B + 2 * D_MODEL

kernel_name = 'hgrn2_conformer_parallel_dit_block'


def _rmsnorm(x, g):
    xf = x.astype(jnp.float32)
    y = xf * lax.rsqrt(jnp.mean(xf * xf, axis=-1, keepdims=True) + EPS)
    return (y * g).astype(x.dtype)


def _modulate(x, g, shift, scale):
    return _rmsnorm(x, g) * (1 + scale) + shift


def _grid_pos_embed(seq_len, dim):
    rows = seq_len // GRID_W
    r, cl = jnp.meshgrid(jnp.arange(rows), jnp.arange(GRID_W), indexing='ij')
    quarter = dim // 4
    omega = 1.0 / (POS_BASE ** (jnp.arange(quarter, dtype=jnp.float32) / quarter))
    ang_r = r.reshape(-1, 1).astype(jnp.float32) * omega
    ang_c = cl.reshape(-1, 1).astype(jnp.float32) * omega
    return jnp.concatenate([jnp.sin(ang_r), jnp.cos(ang_r), jnp.sin(ang_c), jnp.cos(ang_c)], axis=-1)


def _split_proj(z):
    idx = np.cumsum([D_A, D_A, D_A, D_A, D_A, 2 * D_B, D_MODEL])
    return jnp.split(z, idx, axis=-1)


def _heads(t):
    b, l, _ = t.shape
    return t.reshape(b, l, A_HEADS, A_HEAD_DIM).transpose(0, 2, 1, 3).astype(jnp.float32)


def _forget(f_logit, lb):
    log_f = jnp.logaddexp(jnp.log(lb), jnp.log1p(-lb) + jax.nn.log_sigmoid(f_logit))
    k = (1.0 - lb) * jax.nn.sigmoid(-f_logit)
    return log_f, k


def _chunk_scan(q, k, v, log_f, s0):
    b, h, seq_len, dk = q.shape
    dv = v.shape[-1]
    n = seq_len // CHUNK

    def to_chunks(t):
        return jnp.moveaxis(t.reshape(b, h, n, CHUNK, t.shape[-1]), 2, 0)

    causal = jnp.tril(jnp.ones((CHUNK, CHUNK), dtype=bool))[:, :, None]

    def step(s, inp):
        qc, kc, vc, gc = inp
        bcum = jnp.cumsum(gc, axis=2)
        diff = bcum[:, :, :, None, :] - bcum[:, :, None, :, :]
        decay = jnp.exp(jnp.where(causal, diff, -jnp.inf))
        scores = jnp.einsum('bhtk,bhsk,bhtsk->bhts', qc, kc, decay)
        o = (jnp.einsum('bhts,bhsv->bhtv', scores, vc)
             + jnp.einsum('bhtk,bhkv->bhtv', qc * jnp.exp(bcum), s))
        b_last = bcum[:, :, -1:, :]
        s_new = (jnp.exp(b_last[:, :, 0, :])[..., None] * s
                 + jnp.einsum('bhsk,bhsv->bhkv', kc * jnp.exp(b_last - bcum), vc))
        return s_new, o

    s_fin, o = lax.scan(step, s0, (to_chunks(q), to_chunks(k), to_chunks(v), to_chunks(log_f)))
    o = jnp.moveaxis(o, 0, 2).reshape(b, h, seq_len, dv)
    return o, s_fin


def _hgrn2_bidir(q, v, f_fw, f_bw, lb, s_fw, s_bw):
    q = _heads(jax.nn.silu(q))
    v = _heads(v)
    lf_fw, k_fw = _forget(_heads(f_fw), lb[0])
    lf_bw, k_bw = _forget(_heads(f_bw), lb[1])
    o_fw, s_fw = _chunk_scan(q, k_fw, v, lf_fw, s_fw)
    flip = lambda t: jnp.flip(t, axis=2)
    o_bw, s_bw = _chunk_scan(flip(q), flip(k_bw), flip(v), flip(lf_bw), s_bw)
    return o_fw + flip(o_bw), s_fw, s_bw


def _hgrn2_readout(o, og, g_onorm, w_a):
    bsz, _, seq_len, _ = o.shape
    o = o * lax.rsqrt(jnp.mean(o * o, axis=-1, keepdims=True) + EPS)
    o = o.transpose(0, 2, 1, 3).reshape(bsz, seq_len, D_A) * g_onorm
    return (o.astype(og.dtype) * jax.nn.silu(og)) @ w_a


def _conformer_conv(u, conv_w, conv_b, ln_g, ln_b, w_b):
    a, g = jnp.split(u, 2, axis=-1)
    h = a * jax.nn.sigmoid(g)
    h = lax.conv_general_dilated(h, conv_w.astype(h.dtype), window_strides=(1,),
                                 padding=[(CONV_PAD, CONV_PAD)],
                                 dimension_numbers=('NWC', 'WIO', 'NWC'),
                                 feature_group_count=D_B) + conv_b
    hf = h.astype(jnp.float32)
    mu = jnp.mean(hf, axis=-1, keepdims=True)
    var = jnp.mean(jnp.square(hf - mu), axis=-1, keepdims=True)
    hn = (hf - mu) * lax.rsqrt(var + EPS) * ln_g + ln_b
    return jax.nn.silu(hn).astype(u.dtype) @ w_b


def _merge(ya, yb, ga, gb, w_o):
    return (jax.nn.sigmoid(ga) * ya + jax.nn.sigmoid(gb) * yb) @ w_o


def _mixer(h, hc, w_in, lb, g_onorm, conv_w, conv_b, ln_g, ln_b, w_a, w_b, w_o, ctx_out):
    q, v, f_fw, f_bw, og, glu, ga, gb = _split_proj(h @ w_in)
    qc, vc, f_fwc, f_bwc, ogc, gluc, gac, gbc = _split_proj(hc @ w_in)
    s0 = jnp.zeros((h.shape[0], A_HEADS, A_HEAD_DIM, A_HEAD_DIM), jnp.float32)
    oc, s_fw, s_bw = _hgrn2_bidir(qc, vc, f_fwc, f_bwc, lb, s0, s0)
    o, _, _ = _hgrn2_bidir(q, v, f_fw, f_bw, lb, s_fw, s_bw)
    y = _merge(_hgrn2_readout(o, og, g_onorm, w_a),
               _conformer_conv(glu, conv_w, conv_b, ln_g, ln_b, w_b), ga, gb, w_o)
    if not ctx_out:
        return y, None
    yc = _merge(_hgrn2_readout(oc, ogc, g_onorm, w_a),
                _conformer_conv(gluc, conv_w, conv_b, ln_g, ln_b, w_b), gac, gbc, w_o)
    return y, yc


def _swiglu(h, w_gate, w_up, w_down):
    return (jax.nn.silu(h @ w_gate) * (h @ w_up)) @ w_down


def setup_inputs(seed: int = 0) -> dict:
    key = jax.random.key(seed)
    ks = jax.random.split(key, 24)
    f32 = jnp.float32
    nrm = lambda k, shape, s: jax.random.normal(k, shape, f32) * s
    return {
        'x': nrm(ks[0], (BATCH, SEQ, D_MODEL), 1.0),
        'c': nrm(ks[1], (BATCH, D_MODEL), 1.0),
        'ctx': nrm(ks[2], (BATCH, CTX_LEN, D_MODEL), 1.0),
        'c_ctx': nrm(ks[3], (D_MODEL,), 1.0),
        'w_mod': nrm(ks[4], (DEPTH, D_MODEL, MOD_DIM), 0.5 * D_MODEL ** -0.5),
        'b_mod': nrm(ks[5], (DEPTH, MOD_DIM), 0.01),
        'g_norm1': 1.0 + nrm(ks[6], (DEPTH, D_MODEL), 0.01),
        'w_in': nrm(ks[7], (DEPTH, D_MODEL, P_IN), D_MODEL ** -0.5),
        'lb_logits': nrm(ks[8], (DEPTH, 2, D_A), 1.0),
        'g_onorm': 1.0 + nrm(ks[9], (DEPTH, D_A), 0.01),
        'w_a': nrm(ks[10], (DEPTH, D_A, D_MODEL), D_A ** -0.5),
        'conv_w': nrm(ks[11], (DEPTH, CONV_K, 1, D_B), CONV_K ** -0.5),
        'conv_b': nrm(ks[12], (DEPTH, D_B), 0.01),
        'ln_g': 1.0 + nrm(ks[13], (DEPTH, D_B), 0.01),
        'ln_b': nrm(ks[14], (DEPTH, D_B), 0.01),
        'w_b': nrm(ks[15], (DEPTH, D_B, D_MODEL), D_B ** -0.5),
        'w_o': nrm(ks[16], (DEPTH, D_MODEL, D_MODEL), D_MODEL ** -0.5),
        'g_norm2': 1.0 + nrm(ks[17], (DEPTH, D_MODEL), 0.01),
        'w_ffn_gate': nrm(ks[18], (DEPTH, D_MODEL, D_FF), D_MODEL ** -0.5),
        'w_ffn_up': nrm(ks[19], (DEPTH, D_MODEL, D_FF), D_MODEL ** -0.5),
        'w_ffn_down': nrm(ks[20], (DEPTH, D_FF, D_MODEL), D_FF ** -0.5),
        'g_final': 1.0 + nrm(ks[21], (D_MODEL,), 0.01),
    }


def reference(x, c, ctx, c_ctx, w_mod, b_mod, g_norm1, w_in, lb_logits, g_onorm, w_a, conv_w,
              conv_b, ln_g, ln_b, w_b, w_o, g_norm2, w_ffn_gate, w_ffn_up, w_ffn_down, g_final):
    seq_len = x.shape[1]
    x = x + _grid_pos_embed(seq_len, x.shape[-1]).astype(x.dtype)
    lb_all = jnp.cumsum(jax.nn.softmax(lb_logits.astype(jnp.float32), axis=0), axis=0)
    lb_all = lb_all - lb_all[0:1]
    c_act = jax.nn.silu(c)
    cc_act = jax.nn.silu(c_ctx)
    for layer in range(DEPTH):
        last = layer == DEPTH - 1
        lb = lb_all[layer].reshape(2, A_HEADS, 1, A_HEAD_DIM)
        mod = c_act @ w_mod[layer] + b_mod[layer]
        mod_c = cc_act @ w_mod[layer] + b_mod[layer]
        sh1, sc1, gt1, sh2, sc2, gt2 = [m[:, None, :] for m in jnp.split(mod, 6, axis=-1)]
        sh1c, sc1c, gt1c, sh2c, sc2c, gt2c = jnp.split(mod_c, 6)
        h = _modulate(x, g_norm1[layer], sh1, sc1)
        hc = _modulate(ctx, g_norm1[layer], sh1c, sc1c)
        y, yc = _mixer(h, hc, w_in[layer], lb, g_onorm[layer], conv_w[layer], conv_b[layer],
                       ln_g[layer], ln_b[layer], w_a[layer], w_b[layer], w_o[layer], not last)
        x = x + gt1 * y
        h = _modulate(x, g_norm2[layer], sh2, sc2)
        x = x + gt2 * _swiglu(h, w_ffn_gate[layer], w_ffn_up[layer], w_ffn_down[layer])
        if not last:
            ctx = ctx + gt1c * yc
            hc = _modulate(ctx, g_norm2[layer], sh2c, sc2c)
            ctx = ctx + gt2c * _swiglu(hc, w_ffn_gate[layer], w_ffn_up[layer], w_ffn_down[layer])
    return _rmsnorm(x, g_final)
```

```python
import contextlib
import numpy as np
import concourse.bass as bass
import concourse.mybir as mybir
from concourse.bass_utils import run_bass_kernel_spmd

F32 = mybir.dt.float32
BF16 = mybir.dt.bfloat16
I32 = mybir.dt.int32
U8 = mybir.dt.uint8
AF = mybir.ActivationFunctionType
OP = mybir.AluOpType
AX = mybir.AxisListType

D = 4096
NKC = 32
DEPTH = 2
NCTX = 64
NLAT = 2048
NT = NCTX + NLAT
CH = 32
NCH = NT // CH
CPB = 128 // CH
NCC = NCTX // CH
NBLK = (NT + 127) // 128
DA = 2048
NH = 16
DB = 2048
DFF = 11008
PIN = 22528
EPS = 1e-6
TT = [(0, 704), (704, 704), (1408, 704)]


def nchunks(t0, tn):
    out = []
    o = 0
    while o < tn:
        n = min(512, tn - o)
        out.append((t0 + o, n))
        o += n
    return out


class Buf:
    __slots__ = ("w", "r")

    def __init__(self):
        self.w = None
        self.r = {}


class Em:
    def __init__(self, nc, es):
        self.nc = nc
        self.eng = {"pe": nc.tensor, "act": nc.scalar, "dve": nc.vector, "pool": nc.gpsimd, "sp": nc.sync}
        self.sem = {k: es.enter_context(nc.semaphore("s_" + k)) for k in ["pe", "act", "dve", "pool"]}
        self.cnt = {k: 0 for k in self.sem}
        self.dpool = {"sp": list(range(0, 16)), "act": list(range(16, 22)), "pool": list(range(22, 28))}
        self.dsem = [es.enter_context(nc.semaphore("d%d" % i)) for i in range(28)]
        self.dcnt = [0] * 28
        self.dnext = {"sp": 0, "act": 0, "pool": 0}
        self.waited = {e: {} for e in self.eng}
        self.ccsem = es.enter_context(nc.semaphore("ccs"))
        self.cccnt = 0

    def _semof(self, sk):
        if isinstance(sk, str):
            return self.sem[sk]
        return self.ccsem if sk[0] == "c" else self.dsem[sk[1]]

    def _wait(self, e, ev):
        if ev is None:
            return
        sk, val, _ = ev
        if self.waited[e].get(sk, 0) >= val:
            return
        self.eng[e].wait_ge(self._semof(sk), val)
        self.waited[e][sk] = val

    def deps(self, e, reads, writes):
        for b in reads:
            if b.w is not None:
                self._wait(e, b.w)
        for b in writes:
            if b.w is not None and b.w[2] != e:
                self._wait(e, b.w)
            for re_, ev in b.r.items():
                if re_ != e:
                    self._wait(e, ev)

    def done(self, e, ins, reads, writes):
        ins.then_inc(self.sem[e], 1)
        self.cnt[e] += 1
        ev = (e, self.cnt[e], e)
        self.waited[e][e] = max(self.waited[e].get(e, 0), 0)
        for b in writes:
            b.w = ev
            b.r = {}
        for b in reads:
            b.r[e] = ev
        return ev

    def op(self, e, fn, reads=(), writes=()):
        self.deps(e, reads, writes)
        ins = fn(self.eng[e])
        return self.done(e, ins, reads, writes)

    def dma(self, q, out, in_, reads=(), writes=(), **kw):
        pool = self.dpool[q]
        i = pool[self.dnext[q] % len(pool)]
        self.dnext[q] += 1
        sk = ("d", i)
        if self.dcnt[i] > 0:
            self._wait(q, (sk, 16 * self.dcnt[i], sk))
        self.deps(q, reads, writes)
        self.eng[q].dma_start(out=out, in_=in_, **kw).then_inc(self.dsem[i], 16)
        self.dcnt[i] += 1
        ev = (sk, 16 * self.dcnt[i], sk)
        for b in writes:
            b.w = ev
            b.r = {}
        for b in reads:
            b.r[sk] = ev
        return ev

    def cc(self, ins_ap, outs_ap, reads, writes):
        q = "pool"
        sk = ("c", 0)
        self.deps(q, reads, writes)
        self.nc.gpsimd.collective_compute("AllGather", OP.bypass, replica_groups=[list(range(8))],
                                          ins=[ins_ap], outs=[outs_ap]).then_inc(self.ccsem, 1)
        self.cccnt += 1
        ev = (sk, self.cccnt, sk)
        for b in writes:
            b.w = ev
            b.r = {}
        for b in reads:
            b.r[sk] = ev

    def barrier(self):
        for e in self.eng:
            for k in self.sem:
                if k != e and self.cnt[k] > 0:
                    self._wait(e, (k, self.cnt[k], k))
            for i in range(len(self.dsem)):
                if self.dcnt[i] > 0:
                    sk = ("d", i)
                    self._wait(e, (sk, 16 * self.dcnt[i], sk))
            if self.cccnt > 0:
                self._wait(e, (("c", 0), self.cccnt, ("c", 0)))


class T:
    def __init__(self, t, buf=None):
        self.t = t
        self.b = buf if buf is not None else Buf()

    def __getitem__(self, k):
        return self.t[k]


def build(dbg=None):
    nc = bass.Bass("TRN2", target_bir_lowering=False)
    es = contextlib.ExitStack()
    with es:
        _emit(nc, es, dbg)
    return nc


def _emit(nc, es, dbg):
    em = Em(nc, es)

    def din(name, shape, dt=F32):
        return nc.dram_tensor(name, list(shape), dt, kind="ExternalInput").ap()

    _scr = {}

    def dscr(name, shape, dt=F32):
        t = T(nc.dram_tensor(name, list(shape), dt, kind="Internal").ap())
        _scr[name] = (t, list(shape), dt)
        return t

    x_lat = din("x_lat", [NLAT, D])
    x_ctx = din("x_ctx", [NCTX, D])
    posv = din("posv", [2, NLAT])
    csT_in = din("csT", [128, NKC, 4])
    flags_in = din("flags", [128, 80])
    wmod = din("wmod", [DEPTH, D, 3072])
    bmod_in = din("bmod", [128, DEPTH, 24])
    g1_in = din("g1", [128, DEPTH, NKC])
    g2_in = din("g2", [128, DEPTH, NKC])
    gfin_in = din("gfin", [128, NKC])
    w_in = din("w_in", [DEPTH, D, PIN])
    lbl_in = din("lbl", [128, DEPTH, 2, NH])
    gon_in = din("gon", [128, DEPTH, NH])
    w_a = din("w_a", [DEPTH, DA, D])
    cw_in = din("cw", [128, DEPTH, 16, 31])
    cb_in = din("cb", [128, DEPTH, 16])
    lng_in = din("lng", [128, DEPTH, 16])
    lnb_in = din("lnb", [128, DEPTH, 16])
    w_b = din("w_b", [DEPTH, DB, D])
    w_o = din("w_o", [DEPTH, D, D])
    w_g = din("w_g", [DEPTH, D, DFF])
    w_u = din("w_u", [DEPTH, D, DFF])
    w_d = din("w_d", [DEPTH, DFF, D])
    out = nc.dram_tensor("out", [NLAT, D], F32, kind="ExternalOutput").ap()

    XT = dscr("XT", [D, NT])
    HT = dscr("HT", [D, NT], BF16)
    QS = dscr("QS", [DA, NT])
    VF = dscr("VF", [DA, NT], BF16)
    FL = dscr("FL", [2 * DA, NT])
    OG = dscr("OG", [DA, NT], BF16)
    HG = dscr("HG", [DB, NT])
    GA = dscr("GA", [D, NT], BF16)
    GB = dscr("GB", [D, NT], BF16)
    QT = dscr("QT", [2 * DA, NT], BF16)
    O0 = dscr("O0", [DA, NT])
    MA = dscr("MA", [DA, NT], BF16)
    MB = dscr("MB", [DB, NT], BF16)
    MM = dscr("MM", [D, NT], BF16)
    ACTT = dscr("ACTT", [DFF, NT], BF16)
    YT = dscr("YT", [D, NLAT])
    XROWS = 8192 + 128 + 2048
    XI = dscr("XI", [XROWS, 128])
    XO = dscr("XO", [8 * XROWS, 128])
    XMI = dscr("XMI", [128, 144])
    XMO = dscr("XMO", [8 * 128, 144])

    _uid = [0]

    _live = {}

    @contextlib.contextmanager
    def _trk(nm, nbytes):
        _live[nm] = nbytes
        try:
            yield
        finally:
            _live.pop(nm, None)

    def sb(name, shape, dt=F32, stack=None):
        _uid[0] += 1
        nm = "%s_u%d" % (name, _uid[0])
        nb = int(np.prod(shape[1:])) * (4 if dt in (F32, I32) else 2 if dt == BF16 else 1)
        try:
            t = (stack or es).enter_context(nc.sbuf_tensor(nm, list(shape), dt))
        except AssertionError:
            print("LIVE:", sum(_live.values()), sorted(_live.items(), key=lambda kv: -kv[1])[:25])
            raise
        (stack or es).enter_context(_trk(nm, nb))
        return T(t)

    ps = [T(es.enter_context(nc.psum_tensor("ps%d" % i, [128, 512], F32))) for i in range(8)]
    ones_f = sb("ones_f", [128, 128])
    ident_f = sb("ident_f", [128, 128])
    ident_b = sb("ident_b", [128, 128], BF16)
    zeros_f = sb("zeros_f", [128, 128])
    mask_fw = sb("mask_fw", [128, 128], U8)
    mask_bw = sb("mask_bw", [128, 128], U8)
    flags = sb("flags_sb", [128, 80])
    modv = sb("modv", [128, DEPTH, 6, NKC, 2])
    A1 = sb("A1", [128, DEPTH, 2, NKC, 2])
    B1 = sb("B1", [128, DEPTH, 2, NKC, 2])
    g1 = sb("g1s", [128, DEPTH, NKC])
    g2 = sb("g2s", [128, DEPTH, NKC])
    gfin = sb("gfins", [128, NKC])
    lb = sb("lbs", [128, DEPTH, 2, NH])
    oml = sb("omls", [128, DEPTH, 2, NH])
    gon = sb("gons", [128, DEPTH, NH])
    cw = sb("cws", [128, DEPTH, 16, 31])
    cb = sb("cbs", [128, DEPTH, 16])
    lng = sb("lngs", [128, DEPTH, 16])
    lnb = sb("lnbs", [128, DEPTH, 16])
    CM = sb("CM", [128, 2, NH, NCH])
    epsc = sb("epsc", [128, 1])

    V, A, P_, G = "dve", "act", "pool", "pe"

    def load(dst, src_ap, q="sp", dst_ap=None, rd=()):
        em.dma(q, dst_ap if dst_ap is not None else dst[:], src_ap, reads=[], writes=[dst.b])

    em.op(V, lambda e: e.memset(ones_f[:], 1.0), writes=[ones_f.b])
    em.op(V, lambda e: e.memset(zeros_f[:], 0.0), writes=[zeros_f.b])
    em.op(V, lambda e: e.memset(epsc[:], EPS), writes=[epsc.b])
    with contextlib.ExitStack() as st:
        it = sb("it_i", [128, 128], I32, st)
        itf = sb("it_f", [128, 128], F32, st)
        em.op(P_, lambda e: e.iota(it[:], [[1, 128]], base=0, channel_multiplier=-1), writes=[it.b])
        em.op(V, lambda e: e.tensor_copy(out=itf[:], in_=it[:]), reads=[it.b], writes=[itf.b])
        em.op(V, lambda e: e.tensor_single_scalar(out=ident_f[:], in_=itf[:], scalar=0.0, op=OP.is_equal),
              reads=[itf.b], writes=[ident_f.b])
        em.op(V, lambda e: e.tensor_copy(out=ident_b[:], in_=ident_f[:]), reads=[ident_f.b], writes=[ident_b.b])
        ch_t = sb("ch_t", [128, 128], F32, st)
        ch_s = sb("ch_s", [128, 1], F32, st)
        pidx = sb("pidx", [128, 1], I32, st)
        pf = sb("pf", [128, 1], F32, st)
        tmpm = sb("tmpm", [128, 128], F32, st)
        tmp2 = sb("tmp2", [128, 128], F32, st)
        tj_i = sb("tj_i", [128, 128], I32, st)
        tj = sb("tj", [128, 128], F32, st)
        em.op(P_, lambda e: e.iota(tj_i[:], [[1, 128]], base=0, channel_multiplier=0), writes=[tj_i.b])
        em.op(V, lambda e: e.tensor_copy(out=tj[:], in_=tj_i[:]), reads=[tj_i.b], writes=[tj.b])
        em.op(V, lambda e: e.tensor_single_scalar(out=ch_t[:], in_=tj[:], scalar=31.5, op=OP.is_gt),
              reads=[tj.b], writes=[ch_t.b])
        for thr in (63.5, 95.5):
            em.op(V, lambda e: e.scalar_tensor_tensor(out=ch_t[:], in0=tj[:], scalar=thr, in1=ch_t[:], op0=OP.is_gt, op1=OP.add),
                  reads=[tj.b, ch_t.b], writes=[ch_t.b])
        em.op(P_, lambda e: e.iota(pidx[:], [[1, 1]], base=0, channel_multiplier=1), writes=[pidx.b])
        em.op(V, lambda e: e.tensor_copy(out=pf[:], in_=pidx[:]), reads=[pidx.b], writes=[pf.b])
        em.op(V, lambda e: e.tensor_single_scalar(out=ch_s[:], in_=pf[:], scalar=31.5, op=OP.is_gt),
              reads=[pf.b], writes=[ch_s.b])
        for thr in (63.5, 95.5):
            em.op(V, lambda e: e.scalar_tensor_tensor(out=ch_s[:], in0=pf[:], scalar=thr, in1=ch_s[:], op0=OP.is_gt, op1=OP.add),
                  reads=[pf.b, ch_s.b], writes=[ch_s.b])
        em.op(V, lambda e: e.tensor_scalar(out=tmpm[:], in0=ch_t[:], scalar1=ch_s[:, 0:1], scalar2=None,
                                           op0=OP.is_equal), reads=[ch_t.b, ch_s.b], writes=[tmpm.b])
        em.op(V, lambda e: e.tensor_single_scalar(out=tmp2[:], in_=itf[:], scalar=-0.5, op=OP.is_gt),
              reads=[itf.b], writes=[tmp2.b])
        em.op(V, lambda e: e.tensor_tensor(out=tmp2[:], in0=tmp2[:], in1=tmpm[:], op=OP.mult),
              reads=[tmp2.b, tmpm.b], writes=[tmp2.b])
        em.op(V, lambda e: e.tensor_copy(out=mask_fw[:], in_=tmp2[:]), reads=[tmp2.b], writes=[mask_fw.b])
        em.op(V, lambda e: e.tensor_single_scalar(out=tmp2[:], in_=itf[:], scalar=0.5, op=OP.is_lt),
              reads=[itf.b, mask_fw.b], writes=[tmp2.b])
        em.op(V, lambda e: e.tensor_tensor(out=tmp2[:], in0=tmp2[:], in1=tmpm[:], op=OP.mult),
              reads=[tmp2.b, tmpm.b], writes=[tmp2.b])
        em.op(V, lambda e: e.tensor_copy(out=mask_bw[:], in_=tmp2[:]), reads=[tmp2.b], writes=[mask_bw.b])
        em.barrier()
    em.op(V, lambda e: e.memset(ones_f[:], 1.0), writes=[ones_f.b])

    for dst, src in [(flags, flags_in), (g1, g1_in), (g2, g2_in), (gfin, gfin_in), (gon, gon_in), (cw, cw_in),
                     (cb, cb_in), (lng, lng_in), (lnb, lnb_in)]:
        load(dst, src)

    with contextlib.ExitStack() as st:
        lbl = sb("lbl", [128, DEPTH, 2, NH], F32, st)
        dl = sb("dl", [128, 2, NH], F32, st)
        load(lbl, lbl_in)
        em.op(V, lambda e: e.memset(lb[:], 0.0), writes=[lb.b])
        em.op(V, lambda e: e.tensor_tensor(out=dl[:], in0=lbl[:, 1], in1=lbl[:, 0], op=OP.subtract),
              reads=[lbl.b], writes=[dl.b])
        em.op(A, lambda e: e.activation(out=lb[:, 1], in_=dl[:], func=AF.Sigmoid), reads=[dl.b], writes=[lb.b])
        em.op(V, lambda e: e.tensor_scalar(out=oml[:], in0=lb[:], scalar1=-1.0, scalar2=1.0, op0=OP.mult, op1=OP.add),
              reads=[lb.b], writes=[oml.b])
        em.barrier()

    with contextlib.ExitStack() as st:
        csT = sb("csT_sb", [128, NKC, 4], F32, st)
        sg = sb("cs_sg", [128, NKC, 4], F32, st)
        bm = sb("bm_sb", [128, DEPTH, 24], F32, st)
        mods = sb("mods", [128, DEPTH, 24, 3], F32, st)
        wst = [sb("wm%d" % i, [128, NKC, 256], F32, st) for i in range(2)]
        load(csT, csT_in)
        load(bm, bmod_in)
        em.op(A, lambda e: e.activation(out=sg[:], in_=csT[:], func=AF.Sigmoid), reads=[csT.b], writes=[sg.b])
        em.op(V, lambda e: e.tensor_tensor(out=csT[:], in0=csT[:], in1=sg[:], op=OP.mult), reads=[csT.b, sg.b],
              writes=[csT.b])
        gi = 0
        for l in range(DEPTH):
            for g in range(12):
                w = wst[gi % 2]
                load(w, wmod[l, :, g * 256:(g + 1) * 256].rearrange("(k p) c -> p k c", p=128))
                for c2 in range(2):
                    ct = g * 2 + c2
                    pt = ps[gi % 2 * 2 + c2]
                    em.deps(G, [w.b, csT.b], [pt.b])
                    for kc in range(NKC):
                        ins = nc.tensor.matmul(pt[:, 0:4], w[:, kc, c2 * 128:(c2 + 1) * 128], csT[:, kc, :],
                                               start=(kc == 0), stop=(kc == NKC - 1))
                    em.done(G, ins, [w.b, csT.b], [pt.b])
                    em.op(V, lambda e: e.tensor_scalar(out=mods[:, l, ct, :], in0=pt[:, 0:3], scalar1=bm[:, l, ct:ct + 1],
                                                       scalar2=None, op0=OP.add), reads=[pt.b, bm.b], writes=[mods.b])
                gi += 1
        em.dma("sp", XMI[:], mods[:].rearrange("p l c t -> p (l c t)"), reads=[mods.b], writes=[XMI.b])
        em.cc(XMI[:], XMO[:], [XMI.b], [XMO.b])
        em.barrier()
        mall = sb("mall", [128, 8, DEPTH, 24, 3], F32, st)
        load(mall, XMO[:].rearrange("(r p) c -> p r c", p=128), dst_ap=mall[:].rearrange("p r l c t -> p r (l c t)"),
             rd=[XMO.b])
        for l in range(DEPTH):
            mv = modv[:, l].rearrange("p g k t -> p (g k) t").rearrange("p (r c) t -> p r c t", c=24)
            src = mall[:, :, l]
            em.op(V, lambda e: e.tensor_scalar(out=mv[:, :, :, 0], in0=src[:, :, :, 0], scalar1=flags[:, 0:1], scalar2=None,
                                               op0=OP.mult), reads=[mall.b, flags.b], writes=[modv.b])
            em.op(V, lambda e: e.scalar_tensor_tensor(out=mv[:, :, :, 0], in0=src[:, :, :, 1], scalar=flags[:, 1:2],
                                                      in1=mv[:, :, :, 0], op0=OP.mult, op1=OP.add),
                  reads=[mall.b, flags.b, modv.b], writes=[modv.b])
            em.op(V, lambda e: e.tensor_copy(out=mv[:, :, :, 1], in_=src[:, :, :, 2]), reads=[mall.b], writes=[modv.b])
            for s, gg, (ish, isc) in [(0, g1, (0, 1)), (1, g2, (3, 4))]:
                for v in range(2):
                    em.op(V, lambda e: e.scalar_tensor_tensor(out=A1[:, l, s, :, v], in0=modv[:, l, isc, :, v], scalar=1.0,
                                                              in1=gg[:, l, :], op0=OP.add, op1=OP.mult),
                          reads=[modv.b, gg.b], writes=[A1.b])
                    em.op(V, lambda e: e.tensor_copy(out=B1[:, l, s, :, v], in_=modv[:, l, ish, :, v]), reads=[modv.b],
                          writes=[B1.b])
        em.barrier()

    def rms_stats(rstd, src_dram, nrows_kc, t_lo, t_n, st, scale):
        chs = nchunks(0, t_n)
        xk = [sb("rs_x%d" % i, [128, t_n], F32, st) for i in range(2)]
        sq = [sb("rs_q%d" % i, [128, t_n], F32, st) for i in range(2)]
        for kc in range(nrows_kc):
            x = xk[kc % 2]
            q = sq[kc % 2]
            load(x, src_dram[kc * 128:(kc + 1) * 128, t_lo:t_lo + t_n], rd=[src_dram.b])
            em.op(A, lambda e: e.activation(out=q[:], in_=x[:], func=AF.Square), reads=[x.b], writes=[q.b])
            for ci, (o, n) in enumerate(chs):
                em.deps(G, [q.b, ones_f.b], [ps[ci].b] if kc == 0 else [])
                ins = nc.tensor.matmul(ps[ci][:, 0:n], ones_f[:], q[:, o:o + n], start=(kc == 0), stop=(kc == nrows_kc - 1))
                em.done(G, ins, [q.b, ones_f.b], [ps[ci].b])
        for ci, (o, n) in enumerate(chs):
            em.op(A, lambda e: e.activation(out=rstd[:, o:o + n], in_=ps[ci][:, 0:n], func=AF.Ln, bias=epsc[:, 0:1],
                                            scale=scale), reads=[ps[ci].b, epsc.b], writes=[rstd.b])
        em.op(A, lambda e: e.activation(out=rstd[:], in_=rstd[:], func=AF.Exp, scale=-0.5), reads=[rstd.b], writes=[rstd.b])

    def norm_stage(l, s):
        with contextlib.ExitStack() as st:
            rstd = sb("n_rstd", [128, NT], F32, st)
            rms_stats(rstd, XT, NKC, 0, NT, st, 1.0 / D)
            xk = [sb("n_x%d" % i, [128, NT], F32, st) for i in range(2)]
            hk = [sb("n_h%d" % i, [128, NT], BF16, st) for i in range(2)]
            for kc in range(NKC):
                x = xk[kc % 2]
                h = hk[kc % 2]
                load(x, XT[kc * 128:(kc + 1) * 128, :], rd=[XT.b])
                em.op(V, lambda e: e.tensor_tensor(out=x[:], in0=x[:], in1=rstd[:], op=OP.mult), reads=[x.b, rstd.b],
                      writes=[x.b])
                em.op(A, lambda e: e.activation(out=h[:, NCTX:], in_=x[:, NCTX:], func=AF.Identity,
                                                bias=B1[:, l, s, kc, 0:1], scale=A1[:, l, s, kc, 0:1]),
                      reads=[x.b, A1.b, B1.b], writes=[h.b])
                em.op(A, lambda e: e.activation(out=h[:, 0:NCTX], in_=x[:, 0:NCTX], func=AF.Identity,
                                                bias=B1[:, l, s, kc, 1:2], scale=A1[:, l, s, kc, 1:2]),
                      reads=[x.b, A1.b, B1.b], writes=[h.b])
                em.dma("sp", HT[kc * 128:(kc + 1) * 128, :], h[:], reads=[h.b])
            em.barrier()

    def proj(groups, srcs, t0, tn, out_specs, st):
        chs = nchunks(0, tn)
        nchk = len(chs)
        src_sb = []
        for si, (dr, r0, nk) in enumerate(srcs):
            s_ = sb("pj_src%d" % si, [128, nk, tn], BF16, st)
            for k0 in range(0, nk, 8):
                k1 = min(nk, k0 + 8)
                em.dma("sp", s_[:, k0:k1, :], dr[r0 + k0 * 128:r0 + k1 * 128, t0:t0 + tn].rearrange("(k p) t -> p k t", p=128),
                       writes=[s_.b])
            src_sb.append(s_)
        nterm = len(groups[0]["terms"])
        slot_elems = sum(srcs[si][2] for (si, _, _) in groups[0]["terms"]) * groups[0]["ncols"]
        wf = [sb("pj_wf%d" % i, [128, slot_elems], F32, st) for i in range(2)]
        wb = [sb("pj_wb%d" % i, [128, slot_elems], BF16, st) for i in range(2)]
        otiles = [[sb("pj_o%d_%d" % (i, j), [128, tn], dt, st) for j, dt in enumerate(out_specs)] for i in range(2)]

        def issue_load(gi):
            grp = groups[gi]
            off = 0
            for (si, W, c0) in grp["terms"]:
                nk = srcs[si][2]
                n = nk * grp["ncols"]
                dstv = wf[gi % 2][:, off:off + n].rearrange("p (k c) -> p k c", c=grp["ncols"])
                for k0 in range(0, nk, 16):
                    k1 = min(nk, k0 + 16)
                    em.dma("sp", dstv[:, k0:k1, :], W[k0 * 128:k1 * 128, c0:c0 + grp["ncols"]].rearrange("(k p) c -> p k c", p=128),
                           writes=[wf[gi % 2].b])
                off += n
            em.op(P_, lambda e: e.tensor_copy(out=wb[gi % 2][:, 0:off], in_=wf[gi % 2][:, 0:off]), reads=[wf[gi % 2].b],
                  writes=[wb[gi % 2].b])

        issue_load(0)
        job = 0
        for gi, grp in enumerate(groups):
            if gi + 1 < len(groups):
                issue_load(gi + 1)
            w = wb[gi % 2]
            for ct in range(grp["ncols"] // 128):
                if nterm == 1 and nchk <= 3:
                    base = (job % 2) * 3
                else:
                    base = 0
                psl = []
                off = 0
                for ti, (si, W, c0) in enumerate(grp["terms"]):
                    nk = srcs[si][2]
                    wv = w[:, off:off + nk * grp["ncols"]].rearrange("p (k c) -> p k c", c=grp["ncols"])
                    off += nk * grp["ncols"]
                    pss = [ps[base + ti * nchk + ci] for ci in range(nchk)]
                    em.deps(G, [w.b, src_sb[si].b], [p.b for p in pss])
                    for kc in range(nk):
                        for ci, (o, n) in enumerate(chs):
                            ins = nc.tensor.matmul(pss[ci][:, 0:n], wv[:, kc, ct * 128:(ct + 1) * 128], src_sb[si][:, kc, o:o + n],
                                                   start=(kc == 0), stop=(kc == nk - 1))
                    em.done(G, ins, [w.b, src_sb[si].b], [p.b for p in pss])
                    psl.append(pss)
                grp["evac"](grp["ct0"] + ct, psl, otiles[job % 2], chs)
                job += 1

    with contextlib.ExitStack() as st:
        om = sb("om", [128, 8], F32, st)
        omi = sb("omi", [128, 8], I32, st)
        em.op(P_, lambda e: e.iota(omi[:], [[128, 8]], base=0, channel_multiplier=1), writes=[omi.b])
        em.op(V, lambda e: e.tensor_copy(out=om[:], in_=omi[:]), reads=[omi.b], writes=[om.b])
        em.op(A, lambda e: e.activation(out=om[:], in_=om[:], func=AF.Exp, scale=-float(np.log(10000.0) / 1024.0)),
              reads=[om.b], writes=[om.b])
        pv = sb("pv", [128, 2, NLAT], F32, st)
        load(pv, posv.rearrange("a n -> (a n)").partition_broadcast(128), dst_ap=pv[:].rearrange("p a n -> p (a n)"))
        TWO_PI = float(2 * np.pi)
        pe = sb("pe_t", [128, NKC, 128], F32, st)
        u = sb("pe_u", [128, 8, 128], F32, st)
        ki = sb("pe_ki", [128, 8, 128], I32, st)
        kf = sb("pe_kf", [128, 8, 128], F32, st)

        def posemb_quarter(q, tb):
            a = 0 if q < 2 else 1
            phase = 0.0 if q % 2 == 0 else float(np.pi / 2)
            posb = pv[:, a, tb * 128:(tb + 1) * 128].unsqueeze(1).broadcast_to([128, 8, 128])
            omb = om[:].unsqueeze(2).broadcast_to([128, 8, 128])
            em.op(V, lambda e: e.tensor_tensor(out=u[:], in0=omb, in1=posb, op=OP.mult), reads=[om.b, pv.b], writes=[u.b])
            em.op(V, lambda e: e.tensor_scalar(out=u[:], in0=u[:], scalar1=phase + float(np.pi), scalar2=1.0 / TWO_PI,
                                               op0=OP.add, op1=OP.mult), reads=[u.b], writes=[u.b])
            em.op(V, lambda e: e.tensor_copy(out=ki[:], in_=u[:]), reads=[u.b], writes=[ki.b])
            em.op(V, lambda e: e.tensor_copy(out=kf[:], in_=ki[:]), reads=[ki.b], writes=[kf.b])
            em.op(V, lambda e: e.tensor_tensor(out=u[:], in0=u[:], in1=kf[:], op=OP.subtract), reads=[u.b, kf.b], writes=[u.b])
            em.op(V, lambda e: e.scalar_tensor_tensor(out=u[:], in0=u[:], scalar=0.0, in1=u[:], op0=OP.is_lt, op1=OP.add),
                  reads=[u.b], writes=[u.b])
            em.op(V, lambda e: e.tensor_scalar(out=u[:], in0=u[:], scalar1=-0.5, scalar2=TWO_PI, op0=OP.add, op1=OP.mult),
                  reads=[u.b], writes=[u.b])
            em.op(V, lambda e: e.tensor_scalar(out=u[:], in0=u[:], scalar1=-3.1415925, scalar2=3.1415925, op0=OP.max,
                                               op1=OP.min), reads=[u.b], writes=[u.b])
            em.op(A, lambda e: e.activation(out=pe[:, q * 8:(q + 1) * 8, :], in_=u[:], func=AF.Sin), reads=[u.b], writes=[pe.b])

        posemb_quarter(2, 0)
        posemb_quarter(3, 0)
        xin = [sb("i_x%d" % i, [128, D], F32, st) for i in range(2)]
        xo = [sb("i_o%d" % i, [128, NKC, 128], F32, st) for i in range(2)]
        for tb in range(17):
            x = xin[tb % 2]
            o = xo[tb % 2]
            if tb == 16:
                npart = NCTX
                load(x, x_ctx[:, :], dst_ap=x[0:NCTX, :])
            else:
                npart = 128
                load(x, x_lat[tb * 128:(tb + 1) * 128, :])
                posemb_quarter(0, tb)
                posemb_quarter(1, tb)
            for k4 in range(8):
                pt = ps[k4 % 4]
                em.deps(G, [x.b, ident_f.b], [pt.b])
                for j in range(4):
                    kc = k4 * 4 + j
                    ins = nc.tensor.transpose(pt[:, j * 128:j * 128 + npart], x[0:npart, kc * 128:(kc + 1) * 128],
                                              ident_f[0:npart, 0:npart])
                em.done(G, ins, [x.b, ident_f.b], [pt.b])
                src = pt[:].rearrange("p (j t) -> p j t", t=128)[:, :, 0:npart]
                if tb == 16:
                    em.op(A, lambda e: e.activation(out=o[:, k4 * 4:(k4 + 1) * 4, 0:npart], in_=src, func=AF.Copy),
                          reads=[pt.b], writes=[o.b])
                else:
                    em.op(V, lambda e: e.tensor_tensor(out=o[:, k4 * 4:(k4 + 1) * 4, :], in0=src,
                                                       in1=pe[:, k4 * 4:(k4 + 1) * 4, :], op=OP.add),
                          reads=[pt.b, pe.b], writes=[o.b])
            c0 = 0 if tb == 16 else NCTX + tb * 128
            em.dma("sp", XT[:, c0:c0 + npart].rearrange("(k p) t -> p k t", p=128), o[:, :, 0:npart], reads=[o.b],
                   writes=[XT.b])
        em.barrier()

    def layer(l):
        norm_stage(l, 0)
        W = w_in[l]
        for (t0, tn) in TT:
            with contextlib.ExitStack() as st:
                sgt = sb("s1_sg", [128, tn], F32, st)

                def ev_store(dr, row0, func, oi):
                    def f(ct, psl, ot, chs):
                        o = ot[oi]
                        for ci, (off, n) in enumerate(chs):
                            p = psl[0][ci]
                            if func is None:
                                em.op(V, lambda e: e.tensor_copy(out=o[:, off:off + n], in_=p[:, 0:n]), reads=[p.b], writes=[o.b])
                            else:
                                em.op(A, lambda e: e.activation(out=o[:, off:off + n], in_=p[:, 0:n], func=func),
                                      reads=[p.b], writes=[o.b])
                        r = row0 + ct * 128
                        em.dma("sp", dr[r:r + 128, t0:t0 + tn], o[:], reads=[o.b])
                    return f

                def ev_glu(ct, psl, ot, chs):
                    o = ot[0]
                    for ci, (off, n) in enumerate(chs):
                        pa, pg = psl[0][ci], psl[1][ci]
                        em.op(A, lambda e: e.activation(out=sgt[:, off:off + n], in_=pg[:, 0:n], func=AF.Sigmoid),
                              reads=[pg.b], writes=[sgt.b])
                        em.op(V, lambda e: e.tensor_tensor(out=o[:, off:off + n], in0=pa[:, 0:n], in1=sgt[:, off:off + n],
                                                           op=OP.mult), reads=[pa.b, sgt.b], writes=[o.b])
                    em.dma("sp", HG[ct * 128:(ct + 1) * 128, t0:t0 + tn], o[:], reads=[o.b])

                fams = [
                    (0, 16, ev_store(QS, 0, AF.Silu, 0)),
                    (2048, 16, ev_store(VF, 0, None, 1)),
                    (4096, 32, ev_store(FL, 0, None, 0)),
                    (8192, 16, ev_store(OG, 0, AF.Silu, 1)),
                    (14336, 32, ev_store(GA, 0, AF.Sigmoid, 1)),
                    (18432, 32, ev_store(GB, 0, AF.Sigmoid, 1)),
                ]
                groups = []
                for (c0, nct, evf) in fams:
                    for g in range(nct // 2):
                        groups.append(dict(terms=[(0, W, c0 + g * 256)], ncols=256, evac=evf, ct0=g * 2))
                with contextlib.ExitStack() as st2:
                    proj(groups, [(HT, 0, NKC)], t0, tn, [F32, BF16], st2)
                    em.barrier()
                groups = []
                for ct in range(16):
                    groups.append(dict(terms=[(0, W, 10240 + ct * 128), (0, W, 12288 + ct * 128)], ncols=128, evac=ev_glu, ct0=ct))
                with contextlib.ExitStack() as st2:
                    proj(groups, [(HT, 0, NKC)], t0, tn, [F32], st2)
                    em.barrier()
        scan_stage(l)
        em.cc(XI[:], XO[:], [XI.b], [XO.b])
        em.barrier()
        readout_stage(l)
        conv_stage(l)
        for (t0, tn) in TT:
            with contextlib.ExitStack() as st:
                gat = [[sb("mg_g%d_%d" % (i, j), [128, tn], BF16, st) for j in range(2)] for i in range(2)]
                t1 = sb("mg_t1", [128, tn], F32, st)
                t2 = sb("mg_t2", [128, tn], F32, st)
                cnt = [0]

                def ev_merge(ct, psl, ot, chs):
                    o = ot[0]
                    ga_, gb_ = gat[cnt[0] % 2]
                    cnt[0] += 1
                    load(ga_, GA[ct * 128:(ct + 1) * 128, t0:t0 + tn], rd=[GA.b])
                    load(gb_, GB[ct * 128:(ct + 1) * 128, t0:t0 + tn], rd=[GB.b])
                    for ci, (off, n) in enumerate(chs):
                        pa, pb = psl[0][ci], psl[1][ci]
                        em.op(V, lambda e: e.tensor_tensor(out=t1[:, off:off + n], in0=pa[:, 0:n], in1=ga_[:, off:off + n],
                                                           op=OP.mult), reads=[pa.b, ga_.b], writes=[t1.b])
                        em.op(V, lambda e: e.tensor_tensor(out=t2[:, off:off + n], in0=pb[:, 0:n], in1=gb_[:, off:off + n],
                                                           op=OP.mult), reads=[pb.b, gb_.b], writes=[t2.b])
                    em.op(V, lambda e: e.tensor_tensor(out=o[:], in0=t1[:], in1=t2[:], op=OP.add), reads=[t1.b, t2.b],
                          writes=[o.b])
                    em.dma("sp", MM[ct * 128:(ct + 1) * 128, t0:t0 + tn], o[:], reads=[o.b])

                groups = [dict(terms=[(0, w_a[l], g * 256), (1, w_b[l], g * 256)], ncols=256, evac=ev_merge, ct0=g * 2)
                          for g in range(16)]
                proj(groups, [(MA, 0, 16), (MB, 0, 16)], t0, tn, [BF16], st)
                em.barrier()

        def ev_resid(gidx, t0, tn, st, tag):
            xt = [sb("rs_x%s%d" % (tag, i), [128, tn], F32, st) for i in range(2)]
            cnt = [0]

            def f(ct, psl, ot, chs):
                o = ot[0]
                x = xt[cnt[0] % 2]
                cnt[0] += 1
                load(x, XT[ct * 128:(ct + 1) * 128, t0:t0 + tn], rd=[XT.b])
                for ci, (off, n) in enumerate(chs):
                    p = psl[0][ci]
                    em.op(V, lambda e: e.scalar_tensor_tensor(out=o[:, off:off + n], in0=p[:, 0:n],
                                                              scalar=modv[:, l, gidx, ct, 0:1], in1=x[:, off:off + n],
                                                              op0=OP.mult, op1=OP.add), reads=[p.b, x.b, modv.b], writes=[o.b])
                    if t0 == 0 and ci == 0:
                        em.op(V, lambda e: e.scalar_tensor_tensor(out=o[:, 0:NCTX], in0=p[:, 0:NCTX],
                                                                  scalar=modv[:, l, gidx, ct, 1:2], in1=x[:, 0:NCTX],
                                                                  op0=OP.mult, op1=OP.add), reads=[p.b, x.b, modv.b],
                              writes=[o.b])
                em.dma("sp", XT[ct * 128:(ct + 1) * 128, t0:t0 + tn], o[:], reads=[o.b])
            return f

        for (t0, tn) in TT:
            with contextlib.ExitStack() as st:
                evr = ev_resid(2, t0, tn, st, "o")
                groups = [dict(terms=[(0, w_o[l], g * 256)], ncols=256, evac=evr, ct0=g * 2) for g in range(16)]
                proj(groups, [(MM, 0, NKC)], t0, tn, [F32], st)
                em.barrier()
        norm_stage(l, 1)
        for (t0, tn) in TT:
            with contextlib.ExitStack() as st:
                sgt = sb("fa_sg", [128, tn], F32, st)

                def ev_ffa(ct, psl, ot, chs):
                    o = ot[0]
                    for ci, (off, n) in enumerate(chs):
                        pg, pu = psl[0][ci], psl[1][ci]
                        em.op(A, lambda e: e.activation(out=sgt[:, off:off + n], in_=pg[:, 0:n], func=AF.Silu),
                              reads=[pg.b], writes=[sgt.b])
                        em.op(V, lambda e: e.tensor_tensor(out=o[:, off:off + n], in0=pu[:, 0:n], in1=sgt[:, off:off + n],
                                                           op=OP.mult), reads=[pu.b, sgt.b], writes=[o.b])
                    em.dma("sp", ACTT[ct * 128:(ct + 1) * 128, t0:t0 + tn], o[:], reads=[o.b])

                groups = [dict(terms=[(0, w_g[l], ct * 128), (0, w_u[l], ct * 128)], ncols=128, evac=ev_ffa, ct0=ct)
                          for ct in range(DFF // 128)]
                proj(groups, [(HT, 0, NKC)], t0, tn, [BF16], st)
                em.barrier()
        for kh in range(2):
            for (t0, tn) in TT:
                with contextlib.ExitStack() as st:
                    Wd = w_d[l][kh * 5504:(kh + 1) * 5504, :]
                    evr = ev_resid(5, t0, tn, st, "d")
                    groups = [dict(terms=[(0, Wd, ct * 128)], ncols=128, evac=evr, ct0=ct) for ct in range(NKC)]
                    proj(groups, [(ACTT, kh * 5504, 43)], t0, tn, [F32], st)
                    em.barrier()

    def scan_stage(l):
        with contextlib.ExitStack() as st:
            qs = sb("sc_qs", [128, NT], F32, st)
            vf = sb("sc_vf", [128, NT], BF16, st)
            vt = sb("sc_vt", [128, NBLK, 128], BF16, st)
            vt64 = sb("sc_vt64", [64, NT // 64, 128], BF16, st)
            o_sb = sb("sc_o", [128, NT], F32, st)
            hed = sb("sc_hed", [128, 16, 4, 15], F32, st)
            dsum = sb("sc_dsum", [128, 2, 2, NH], F32, st)
            em.op(V, lambda e: e.memset(vt[:], 0.0), writes=[vt.b])
            dd = {}
            for d in range(2):
                dd[d] = dict(
                    fl=sb("sc_fl%d" % d, [128, NT], F32, st), lf=sb("sc_lf%d" % d, [128, NT], F32, st),
                    k=sb("sc_k%d" % d, [128, NT], F32, st), g=sb("sc_g%d" % d, [128, NT], F32, st),
                    e1=sb("sc_e1%d" % d, [128, NT], F32, st),
                    qt=sb("sc_qt%d" % d, [128, NT], BF16, st), kt=sb("sc_kt%d" % d, [128, NT], BF16, st),
                    ktm=sb("sc_ktm%d" % d, [64, NT // 64, 128], BF16, st),
                    ref=sb("sc_ref%d" % d, [128, NCH], F32, st), G=sb("sc_G%d" % d, [128, NCH], F32, st),
                    ER=sb("sc_ER%d" % d, [128, NCH], F32, st), EL=sb("sc_EL%d" % d, [128, NCH], F32, st),
                    ELR=sb("sc_ELR%d" % d, [128, NCH], F32, st), Pp=sb("sc_P%d" % d, [128, NCH], F32, st),
                    S=sb("sc_S%d" % d, [128, 128], F32, st), Sb=sb("sc_Sb%d" % d, [128, 128], BF16, st),
                    tmp=sb("sc_tmp%d" % d, [128, 128], F32, st), Am=sb("sc_A%d" % d, [128, 128], BF16, st),
                )
                em.op(V, lambda e: e.memset(dd[d]["ktm"][:], 0.0), writes=[dd[d]["ktm"].b])
            cmask = sb("cmask", [128, NT], F32, st)
            em.op(V, lambda e: e.memset(cmask[:], 1.0), writes=[cmask.b])
            em.op(V, lambda e: e.memset(cmask[:].rearrange("p (c k) -> p c k", k=CH)[:, :, 0:1], 0.0), writes=[cmask.b])
            gm = sb("sc_gm", [128, NCH], F32, st)
            em.op(V, lambda e: e.memset(gm[:], 1.0), writes=[gm.b])
            em.op(V, lambda e: e.memset(gm[:, 0:1], 0.0), writes=[gm.b])
            em.op(V, lambda e: e.memset(gm[:, NCC:NCC + 1], 0.0), writes=[gm.b])
            zer33 = sb("sc_z33", [128, NCH], F32, st)
            em.op(V, lambda e: e.memset(zer33[:], 0.0), writes=[zer33.b])
            load(hed, HG[:, 0:15].rearrange("(t p) n -> p t n", p=128), dst_ap=hed[:, :, 0, :], rd=[HG.b])
            load(hed, HG[:, 49:64].rearrange("(t p) n -> p t n", p=128), dst_ap=hed[:, :, 1, :], rd=[HG.b])
            load(hed, HG[:, 64:79].rearrange("(t p) n -> p t n", p=128), dst_ap=hed[:, :, 2, :], rd=[HG.b])
            load(hed, HG[:, NT - 15:NT].rearrange("(t p) n -> p t n", p=128), dst_ap=hed[:, :, 3, :], rd=[HG.b])
            em.dma("sp", XI[8320:8320 + 2048, 0:60].rearrange("(t p) n -> p t n", p=128),
                   hed[:].rearrange("p t k n -> p t (k n)"), reads=[hed.b])

            def c3(t):
                return t[:].rearrange("p (c k) -> p c k", k=CH)

            def bc(t):
                return t[:].unsqueeze(2).broadcast_to([128, NCH, CH])

            def transposes(src, dst, blk=128):
                nblk = (NT + blk - 1) // blk
                per = 8
                for gi_, b0 in enumerate(range(0, nblk, per)):
                    pt = ps[6 + gi_ % 2]
                    ptb = pt[:].bitcast(BF16)
                    nb = min(per, nblk - b0)
                    em.deps(G, [src.b, ident_b.b], [pt.b])
                    for j in range(nb):
                        b = b0 + j
                        w_ = min(blk, NT - b * blk)
                        ins = nc.tensor.transpose(ptb[0:w_, j * 128:(j + 1) * 128], src[:, b * blk:b * blk + w_], ident_b[:, :])
                    em.done(G, ins, [src.b, ident_b.b], [pt.b])
                    for j in range(nb):
                        b = b0 + j
                        w_ = min(blk, NT - b * blk)
                        em.op(A, lambda e: e.activation(out=dst[0:w_, b, :], in_=ptb[0:w_, j * 128:(j + 1) * 128], func=AF.Copy),
                              reads=[pt.b], writes=[dst.b])

            for h in range(NH):
                hs = slice(h * 128, (h + 1) * 128)
                load(qs, QS[hs, :], rd=[QS.b])
                load(vf, VF[hs, :], rd=[VF.b])
                transposes(vf, vt)
                transposes(vf, vt64, 64)
                for d in range(2):
                    z = dd[d]
                    load(z["fl"], FL[d * DA + h * 128:d * DA + (h + 1) * 128, :], rd=[FL.b])
                    fl, lf, k, g, e1 = z["fl"], z["lf"], z["k"], z["g"], z["e1"]
                    em.op(A, lambda e: e.activation(out=fl[:], in_=fl[:], func=AF.Sigmoid), reads=[fl.b], writes=[fl.b])
                    em.op(V, lambda e: e.tensor_scalar(out=fl[:], in0=fl[:], scalar1=oml[:, l, d, h:h + 1],
                                                       scalar2=lb[:, l, d, h:h + 1], op0=OP.mult, op1=OP.add),
                          reads=[fl.b, oml.b, lb.b], writes=[fl.b])
                    em.op(A, lambda e: e.activation(out=lf[:], in_=fl[:], func=AF.Ln), reads=[fl.b], writes=[lf.b])
                    em.op(P_, lambda e: e.tensor_scalar(out=k[:], in0=fl[:], scalar1=-1.0, scalar2=1.0, op0=OP.mult, op1=OP.add),
                          reads=[fl.b], writes=[k.b])
                    em.op(V, lambda e: e.tensor_tensor_scan(out=g[:], data0=cmask[:], data1=lf[:], initial=0.0, op0=OP.mult,
                                                            op1=OP.add), reads=[cmask.b, lf.b], writes=[g.b])
                    em.op(V, lambda e: e.tensor_copy(out=z["G"][:], in_=c3(g)[:, :, CH - 1]), reads=[g.b], writes=[z["G"].b])
                    if d == 1:
                        em.op(V, lambda e: e.tensor_tensor(out=c3(g), in0=bc(z["G"]), in1=c3(g), op=OP.subtract),
                              reads=[g.b, z["G"].b], writes=[g.b])
                        em.op(V, lambda e: e.tensor_tensor(out=g[:], in0=g[:], in1=lf[:], op=OP.add), reads=[g.b, lf.b],
                              writes=[g.b])
                    em.op(V, lambda e: e.tensor_copy(out=z["ref"][:], in_=c3(g)[:, :, CH // 2]), reads=[g.b], writes=[z["ref"].b])
                    em.op(V, lambda e: e.tensor_tensor(out=c3(g), in0=c3(g), in1=bc(z["ref"]), op=OP.subtract),
                          reads=[g.b, z["ref"].b], writes=[g.b])
                    em.op(A, lambda e: e.activation(out=e1[:], in_=g[:], func=AF.Exp), reads=[g.b], writes=[e1.b])
                    em.op(V, lambda e: e.tensor_tensor(out=z["qt"][:], in0=qs[:], in1=e1[:], op=OP.mult), reads=[qs.b, e1.b],
                          writes=[z["qt"].b])
                    em.op(A, lambda e: e.activation(out=e1[:], in_=g[:], func=AF.Exp, scale=-1.0), reads=[g.b], writes=[e1.b])
                    em.op(P_, lambda e: e.tensor_tensor(out=z["kt"][:], in0=k[:], in1=e1[:], op=OP.mult), reads=[k.b, e1.b],
                          writes=[z["kt"].b])
                    em.op(A, lambda e: e.activation(out=z["ER"][:], in_=z["ref"][:], func=AF.Exp), reads=[z["ref"].b],
                          writes=[z["ER"].b])
                    em.op(A, lambda e: e.activation(out=z["EL"][:], in_=z["G"][:], func=AF.Exp), reads=[z["G"].b],
                          writes=[z["EL"].b])
                    em.op(V, lambda e: e.tensor_tensor(out=z["ELR"][:], in0=z["G"][:], in1=z["ref"][:], op=OP.subtract),
                          reads=[z["G"].b, z["ref"].b], writes=[z["ELR"].b])
                    em.op(A, lambda e: e.activation(out=z["ELR"][:], in_=z["ELR"][:], func=AF.Exp), reads=[z["ELR"].b],
                          writes=[z["ELR"].b])
                    Pp = z["Pp"]
                    em.op(V, lambda e: e.tensor_tensor_scan(out=Pp[:], data0=gm[:], data1=z["G"][:], initial=0.0, op0=OP.mult,
                                                            op1=OP.add), reads=[gm.b, z["G"].b], writes=[Pp.b])
                    em.op(A, lambda e: e.activation(out=dsum[:, d, 0, h:h + 1], in_=Pp[:, NCC - 1:NCC], func=AF.Exp),
                          reads=[Pp.b], writes=[dsum.b])
                    em.op(A, lambda e: e.activation(out=dsum[:, d, 1, h:h + 1], in_=Pp[:, NCH - 1:NCH], func=AF.Exp),
                          reads=[Pp.b], writes=[dsum.b])
                    cmv = CM[:, d, h, :]
                    if d == 0:
                        em.op(V, lambda e: e.tensor_tensor(out=cmv, in0=Pp[:], in1=z["G"][:], op=OP.subtract),
                              reads=[Pp.b, z["G"].b], writes=[CM.b])
                    else:
                        em.op(V, lambda e: e.tensor_scalar(out=cmv, in0=Pp[:], scalar1=Pp[:, NCH - 1:NCH], scalar2=-1.0,
                                                           op0=OP.subtract, op1=OP.mult), reads=[Pp.b], writes=[CM.b])
                        em.op(V, lambda e: e.tensor_scalar(out=CM[:, d, h, 0:NCC], in0=Pp[:, 0:NCC], scalar1=Pp[:, NCC - 1:NCC],
                                                           scalar2=-1.0, op0=OP.subtract, op1=OP.mult), reads=[Pp.b], writes=[CM.b])
                    em.op(V, lambda e: e.tensor_tensor(out=cmv, in0=cmv, in1=z["ref"][:], op=OP.add), reads=[CM.b, z["ref"].b],
                          writes=[CM.b])
                    em.op(A, lambda e: e.activation(out=cmv, in_=cmv, func=AF.Exp), reads=[CM.b], writes=[CM.b])
                    transposes(z["kt"], z["ktm"], 64)
                    em.dma("sp", QT[d * DA + h * 128:d * DA + (h + 1) * 128, :], z["qt"][:], reads=[z["qt"].b])

                def chunk_update(d, c, first):
                    z = dd[d]
                    b, hf = c // 2, c % 2
                    pu = ps[4 + d]
                    rows = slice(hf * CH, hf * CH + CH)
                    em.deps(G, [z["ktm"].b, vt64.b], [pu.b])
                    ins = nc.tensor.matmul(pu[:, 0:128], z["ktm"][rows, b, :], vt64[rows, b, :], start=True, stop=True)
                    em.done(G, ins, [z["ktm"].b, vt64.b], [pu.b])
                    S = z["S"]
                    if first:
                        em.op(V, lambda e: e.tensor_scalar(out=S[:], in0=pu[:, 0:128], scalar1=z["ELR"][:, c:c + 1], scalar2=None,
                                                           op0=OP.mult), reads=[pu.b, z["ELR"].b], writes=[S.b])
                    else:
                        em.op(V, lambda e: e.tensor_scalar(out=z["tmp"][:], in0=pu[:, 0:128], scalar1=z["ELR"][:, c:c + 1],
                                                           scalar2=None, op0=OP.mult), reads=[pu.b, z["ELR"].b],
                              writes=[z["tmp"].b])
                        em.op(V, lambda e: e.scalar_tensor_tensor(out=S[:], in0=S[:], scalar=z["EL"][:, c:c + 1], in1=z["tmp"][:],
                                                                  op0=OP.mult, op1=OP.add), reads=[S.b, z["EL"].b, z["tmp"].b],
                              writes=[S.b])

                def make_sb(d, cnext):
                    z = dd[d]
                    em.op(A, lambda e: e.activation(out=z["Sb"][:], in_=z["S"][:], func=AF.Identity, scale=z["ER"][:, cnext:cnext + 1]),
                          reads=[z["S"].b, z["ER"].b], writes=[z["Sb"].b])

                def save_summary(d, seg):
                    z = dd[d]
                    r0 = ((d * 2 + seg) * NH + h) * 128
                    em.dma("sp", XI[r0:r0 + 128, :], z["S"][:], reads=[z["S"].b])

                def block(d, b):
                    z = dd[d]
                    w_ = min(128, NT - b * 128)
                    tk = slice(b * 128, b * 128 + w_)
                    psc = ps[0 + d]
                    po = ps[2 + d]
                    em.deps(G, [z["kt"].b, z["qt"].b], [psc.b])
                    ins = nc.tensor.matmul(psc[0:w_, 0:w_], z["kt"][:, tk], z["qt"][:, tk], start=True, stop=True)
                    em.done(G, ins, [z["kt"].b, z["qt"].b], [psc.b])
                    msk = mask_fw if d == 0 else mask_bw
                    em.op(V, lambda e: e.select(out=z["Am"][0:w_, 0:w_], mask=msk[0:w_, 0:w_], on_true=psc[0:w_, 0:w_],
                                                on_false=zeros_f[0:w_, 0:w_]), reads=[psc.b, msk.b, zeros_f.b],
                          writes=[z["Am"].b])
                    chunks = [CPB * b + i for i in range(w_ // CH)]
                    if d == 1:
                        chunks = chunks[::-1]
                    em.deps(G, [vt.b, z["Am"].b], [po.b])
                    ins = nc.tensor.matmul(po[:, 0:w_], vt[0:w_, b, :], z["Am"][0:w_, 0:w_], start=True, stop=True)
                    em.done(G, ins, [vt.b, z["Am"].b], [po.b])
                    for c in chunks:
                        if d == 0:
                            first = c in (0, NCC)
                            last_of_seg = c in (NCC - 1, NCH - 1)
                        else:
                            first = c in (NCC - 1, NCH - 1)
                            last_of_seg = c in (0, NCC)
                        if not first:
                            cs = slice((c % CPB) * CH, (c % CPB) * CH + CH)
                            em.deps(G, [z["Sb"].b, z["qt"].b], [po.b])
                            ins = nc.tensor.matmul(po[:, cs], z["Sb"][:, :], z["qt"][:, c * CH:(c + 1) * CH], start=False, stop=True,
                                                   skip_group_check=True)
                            em.done(G, ins, [z["Sb"].b, z["qt"].b], [po.b])
                        chunk_update(d, c, first)
                        if last_of_seg:
                            save_summary(d, 0 if c < NCC else 1)
                        else:
                            cn = c + 1 if d == 0 else c - 1
                            make_sb(d, cn)
                    if d == 0:
                        em.op(A, lambda e: e.activation(out=o_sb[:, tk], in_=po[:, 0:w_], func=AF.Copy), reads=[po.b],
                              writes=[o_sb.b])
                    else:
                        em.op(V, lambda e: e.tensor_tensor(out=o_sb[:, tk], in0=po[:, 0:w_], in1=o_sb[:, tk], op=OP.add),
                              reads=[po.b, o_sb.b], writes=[o_sb.b])

                for b in range(NBLK):
                    block(0, b)
                for b in range(NBLK - 1, -1, -1):
                    block(1, b)
                em.dma("sp", O0[hs, :], o_sb[:], reads=[o_sb.b])
            em.dma("sp", XI[8192:8320, 0:64], dsum[:].rearrange("p d s h -> p (d s h)"), reads=[dsum.b])
            em.barrier()

    def readout_stage(l):
        with contextlib.ExitStack() as st:
            dall = sb("ro_dall", [128, 8, 2, 2, NH], F32, st)
            load(dall, XO[:].rearrange("(r x) c -> r x c", r=8)[:, 8192:8320, 0:64].rearrange("r p c -> p r c"),
                 dst_ap=dall[:].rearrange("p r d s h -> p r (d s h)"), rd=[XO.b])
            dp = sb("ro_dp", [128, NH], F32, st)
            sall = [sb("ro_sall%d" % i, [128, 8, 2, 2, 128], F32, st) for i in range(2)]
            Acc = sb("ro_acc", [128, 128], F32, st)
            tmp = sb("ro_tmp", [128, 128], F32, st)
            sin = sb("ro_sin", [128, 2, 2, 128], BF16, st)
            o0 = sb("ro_o0", [128, NT], F32, st)
            qt = [sb("ro_qt%d" % d, [128, NT], BF16, st) for d in range(2)]
            qc = [sb("ro_qc%d" % d, [128, NT], BF16, st) for d in range(2)]
            og = sb("ro_og", [128, NT], BF16, st)
            sq = sb("ro_sq", [128, NT], F32, st)
            rstd = sb("ro_rstd", [128, NT], F32, st)
            mo = sb("ro_mo", [128, NT], BF16, st)
            XOv = XO[:].rearrange("(r x) c -> r x c", r=8)
            chs = [(0, NCTX)] + [(NCTX + i * 512, 512) for i in range(4)]
            for h in range(NH):
                hs = slice(h * 128, (h + 1) * 128)
                sa = sall[h % 2]
                for d in range(2):
                    for s in range(2):
                        r0 = ((d * 2 + s) * NH + h) * 128
                        load(sa, XOv[:, r0:r0 + 128, :].rearrange("r p c -> p r c"), dst_ap=sa[:, :, d, s, :], rd=[XO.b])
                load(o0, O0[hs, :], rd=[O0.b])
                load(og, OG[hs, :], rd=[OG.b])
                for d in range(2):
                    load(qt[d], QT[d * DA + h * 128:d * DA + (h + 1) * 128, :], rd=[QT.b])
                for d in range(2):
                    order = list(range(8)) if d == 0 else list(range(7, -1, -1))
                    fpred = 8 if d == 0 else 24
                    for seg in range(2):
                        entries = []
                        if seg == 0:
                            entries = [(0, r, fpred) for r in order]
                        else:
                            entries = [(0, r, 16) for r in order] + [(1, r, fpred) for r in order]
                        em.op(V, lambda e: e.memset(Acc[:], 0.0), writes=[Acc.b])
                        for (es_, r, fc) in entries:
                            fl_ = flags[:, fc + r:fc + r + 1]
                            omf = flags[:, fc + 24 + r:fc + 24 + r + 1]
                            em.op(V, lambda e: e.tensor_scalar(out=dp[:, 0:1], in0=dall[:, r, d, es_, h:h + 1], scalar1=fl_,
                                                               scalar2=omf, op0=OP.mult, op1=OP.add),
                                  reads=[dall.b, flags.b], writes=[dp.b])
                            em.op(P_, lambda e: e.tensor_scalar(out=tmp[:], in0=sa[:, r, d, es_, :], scalar1=fl_, scalar2=None,
                                                                op0=OP.mult), reads=[sa.b, flags.b], writes=[tmp.b])
                            em.op(V, lambda e: e.scalar_tensor_tensor(out=Acc[:], in0=Acc[:], scalar=dp[:, 0:1], in1=tmp[:],
                                                                      op0=OP.mult, op1=OP.add), reads=[Acc.b, dp.b, tmp.b],
                                  writes=[Acc.b])
                        em.op(A, lambda e: e.activation(out=sin[:, d, seg, :], in_=Acc[:], func=AF.Copy), reads=[Acc.b],
                              writes=[sin.b])
                    em.op(V, lambda e: e.tensor_tensor(out=qc[d][:].rearrange("p (c k) -> p c k", k=CH),
                                                       in0=qt[d][:].rearrange("p (c k) -> p c k", k=CH),
                                                       in1=CM[:, d, h, :].unsqueeze(2).broadcast_to([128, NCH, CH]), op=OP.mult),
                          reads=[qt[d].b, CM.b], writes=[qc[d].b])
                for ci, (o, n) in enumerate(chs):
                    seg = 0 if ci == 0 else 1
                    p = ps[ci]
                    em.deps(G, [sin.b, qc[0].b, qc[1].b], [p.b])
                    nc.tensor.matmul(p[:, 0:n], sin[:, 0, seg, :], qc[0][:, o:o + n], start=True, stop=False)
                    ins = nc.tensor.matmul(p[:, 0:n], sin[:, 1, seg, :], qc[1][:, o:o + n], start=False, stop=True)
                    em.done(G, ins, [sin.b, qc[0].b, qc[1].b], [p.b])
                    em.op(V, lambda e: e.tensor_tensor(out=o0[:, o:o + n], in0=p[:, 0:n], in1=o0[:, o:o + n], op=OP.add),
                          reads=[p.b, o0.b], writes=[o0.b])
                em.op(A, lambda e: e.activation(out=sq[:], in_=o0[:], func=AF.Square), reads=[o0.b], writes=[sq.b])
                for ci, (o, n) in enumerate(chs):
                    p = ps[ci]
                    em.deps(G, [sq.b, ones_f.b], [p.b])
                    ins = nc.tensor.matmul(p[:, 0:n], ones_f[:], sq[:, o:o + n], start=True, stop=True)
                    em.done(G, ins, [sq.b, ones_f.b], [p.b])
                    em.op(A, lambda e: e.activation(out=rstd[:, o:o + n], in_=p[:, 0:n], func=AF.Ln, bias=epsc[:, 0:1],
                                                    scale=1.0 / 128.0), reads=[p.b, epsc.b], writes=[rstd.b])
                em.op(A, lambda e: e.activation(out=rstd[:], in_=rstd[:], func=AF.Exp, scale=-0.5), reads=[rstd.b], writes=[rstd.b])
                em.op(V, lambda e: e.tensor_tensor(out=o0[:], in0=o0[:], in1=rstd[:], op=OP.mult), reads=[o0.b, rstd.b],
                      writes=[o0.b])
                em.op(V, lambda e: e.scalar_tensor_tensor(out=mo[:], in0=o0[:], scalar=gon[:, l, h:h + 1], in1=og[:],
                                                          op0=OP.mult, op1=OP.mult), reads=[o0.b, gon.b, og.b], writes=[mo.b])
                em.dma("sp", MA[hs, :], mo[:], reads=[mo.b])
            em.barrier()

    def conv_stage(l):
        with contextlib.ExitStack() as st:
            XOv = XO[:].rearrange("(r x) c -> r x c", r=8)
            halc = sb("cv_halc", [128, 16, 4, 15], F32, st)
            hl = [sb("cv_hl%d" % i, [128, 16, 4, 15], F32, st) for i in range(2)]
            em.op(V, lambda e: e.memset(halc[:], 0.0), writes=[halc.b])
            for r in range(8):
                hr = hl[r % 2]
                load(hr, XOv[r, 8320:8320 + 2048, 0:60].rearrange("(t p) n -> p t n", p=128),
                     dst_ap=hr[:].rearrange("p t k n -> p t (k n)"))
                for kind in range(4):
                    fcol = (56 if kind in (1, 3) else 64) + r
                    em.op(V, lambda e: e.scalar_tensor_tensor(out=halc[:, :, kind, :], in0=hr[:, :, kind, :],
                                                              scalar=flags[:, fcol:fcol + 1], in1=halc[:, :, kind, :],
                                                              op0=OP.mult, op1=OP.add), reads=[hr.b, flags.b, halc.b], writes=[halc.b])
            LMAX = 1024
            call = sb("cv_all", [128, 16, LMAX], F32, st)
            pad = [sb("cv_pad%d" % i, [128, LMAX + 30], F32, st) for i in range(2)]
            acc2 = sb("cv_acc2", [128, LMAX], F32, st)
            acc3 = sb("cv_acc3", [128, LMAX], F32, st)
            sq = [sb("cv_sq%d" % i, [128, 512], F32, st) for i in range(2)]
            mean = sb("cv_mean", [128, 512], F32, st)
            rstd = sb("cv_rstd", [128, 512], F32, st)
            tmpn = [sb("cv_tn%d" % i, [128, 512], F32, st) for i in range(2)]
            mo = [sb("cv_mo%d" % i, [128, 512], BF16, st) for i in range(2)]
            segs = [(0, NCTX, 1, 0), (NCTX, 1024, 3, None), (NCTX + 1024, 1024, None, 2)]
            it = 0
            for (a0, L, lk, rk) in segs:
                for ct in range(16):
                    pd = pad[ct % 2]
                    lo = a0 - (15 if lk is None else 0)
                    hi = a0 + L + (15 if rk is None else 0)
                    em.dma("sp", pd[:, 15 - (a0 - lo):15 + L + (hi - a0 - L)], HG[ct * 128:(ct + 1) * 128, lo:hi], writes=[pd.b])
                    if lk is not None:
                        em.op(P_, lambda e: e.tensor_copy(out=pd[:, 0:15], in_=halc[:, ct, lk, :]), reads=[halc.b], writes=[pd.b])
                    if rk is not None:
                        em.op(P_, lambda e: e.tensor_copy(out=pd[:, 15 + L:30 + L], in_=halc[:, ct, rk, :]), reads=[halc.b],
                              writes=[pd.b])
                    acc = call[:, ct, 0:L]
                    a2 = acc2[:, 0:L]
                    em.op(V, lambda e: e.tensor_scalar(out=acc, in0=pd[:, 0:L], scalar1=cw[:, l, ct, 0:1],
                                                       scalar2=cb[:, l, ct:ct + 1], op0=OP.mult, op1=OP.add),
                          reads=[pd.b, cw.b, cb.b], writes=[call.b])
                    for j in range(1, 22):
                        em.op(V, lambda e: e.scalar_tensor_tensor(out=acc, in0=pd[:, j:j + L], scalar=cw[:, l, ct, j:j + 1],
                                                                  in1=acc, op0=OP.mult, op1=OP.add), reads=[pd.b, cw.b, call.b],
                              writes=[call.b])
                    em.op(P_, lambda e: e.tensor_scalar(out=a2, in0=pd[:, 22:22 + L], scalar1=cw[:, l, ct, 22:23],
                                                        scalar2=None, op0=OP.mult), reads=[pd.b, cw.b], writes=[acc2.b])
                    for j in range(23, 31):
                        em.op(P_, lambda e: e.tensor_scalar(out=acc3[:, 0:L], in0=pd[:, j:j + L], scalar1=cw[:, l, ct, j:j + 1],
                                                            scalar2=None, op0=OP.mult), reads=[pd.b, cw.b], writes=[acc3.b])
                        em.op(P_, lambda e: e.tensor_tensor(out=a2, in0=a2, in1=acc3[:, 0:L], op=OP.add), reads=[acc2.b, acc3.b],
                              writes=[acc2.b])
                    em.op(V, lambda e: e.tensor_tensor(out=acc, in0=acc, in1=a2, op=OP.add), reads=[call.b, acc2.b],
                          writes=[call.b])
                for ci, (o, n) in enumerate(nchunks(0, L)):
                    p1, p2 = ps[(ci % 2) * 2], ps[(ci % 2) * 2 + 1]
                    for ct in range(16):
                        q = sq[it % 2]
                        it += 1
                        em.op(A, lambda e: e.activation(out=q[:, 0:n], in_=call[:, ct, o:o + n], func=AF.Square), reads=[call.b],
                              writes=[q.b])
                        em.deps(G, [call.b, q.b, ones_f.b], [p1.b, p2.b] if ct == 0 else [])
                        nc.tensor.matmul(p1[:, 0:n], ones_f[:], call[:, ct, o:o + n], start=(ct == 0), stop=(ct == 15))
                        ins = nc.tensor.matmul(p2[:, 0:n], ones_f[:], q[:, 0:n], start=(ct == 0), stop=(ct == 15))
                        em.done(G, ins, [call.b, q.b, ones_f.b], [p1.b, p2.b])
                    em.op(A, lambda e: e.activation(out=mean[:, 0:n], in_=p1[:, 0:n], func=AF.Copy, scale=1.0 / DB), reads=[p1.b],
                          writes=[mean.b])
                    em.op(V, lambda e: e.tensor_tensor(out=rstd[:, 0:n], in0=mean[:, 0:n], in1=mean[:, 0:n], op=OP.mult),
                          reads=[mean.b], writes=[rstd.b])
                    em.op(V, lambda e: e.scalar_tensor_tensor(out=rstd[:, 0:n], in0=p2[:, 0:n], scalar=1.0 / DB, in1=rstd[:, 0:n],
                                                              op0=OP.mult, op1=OP.subtract), reads=[p2.b, rstd.b], writes=[rstd.b])
                    em.op(A, lambda e: e.activation(out=rstd[:, 0:n], in_=rstd[:, 0:n], func=AF.Ln, bias=epsc[:, 0:1]),
                          reads=[rstd.b, epsc.b], writes=[rstd.b])
                    em.op(A, lambda e: e.activation(out=rstd[:, 0:n], in_=rstd[:, 0:n], func=AF.Exp, scale=-0.5), reads=[rstd.b],
                          writes=[rstd.b])
                    for ct in range(16):
                        tn_ = tmpn[ct % 2]
                        m_ = mo[ct % 2]
                        em.op(V, lambda e: e.tensor_tensor(out=tn_[:, 0:n], in0=call[:, ct, o:o + n], in1=mean[:, 0:n],
                                                           op=OP.subtract), reads=[call.b, mean.b], writes=[tn_.b])
                        em.op(P_, lambda e: e.tensor_tensor(out=tn_[:, 0:n], in0=tn_[:, 0:n], in1=rstd[:, 0:n], op=OP.mult),
                              reads=[tn_.b, rstd.b], writes=[tn_.b])
                        em.op(A, lambda e: e.activation(out=m_[:, 0:n], in_=tn_[:, 0:n], func=AF.Silu, bias=lnb[:, l, ct:ct + 1],
                                                        scale=lng[:, l, ct:ct + 1]), reads=[tn_.b, lnb.b, lng.b], writes=[m_.b])
                        em.dma("sp", MB[ct * 128:(ct + 1) * 128, a0 + o:a0 + o + n], m_[:, 0:n], reads=[m_.b])
            em.barrier()

    stopped = False
    for l in range(DEPTH):
        layer(l)
        if dbg and dbg.get("stop") == "layer%d" % l:
            stopped = True
            break

    if not stopped:
        with contextlib.ExitStack() as st:
            rstd = sb("f_rstd", [128, NLAT], F32, st)
            rms_stats(rstd, XT, NKC, NCTX, NLAT, st, 1.0 / D)
            xk = [sb("f_x%d" % i, [128, NLAT], F32, st) for i in range(2)]
            for kc in range(NKC):
                x = xk[kc % 2]
                load(x, XT[kc * 128:(kc + 1) * 128, NCTX:NT], rd=[XT.b])
                em.op(V, lambda e: e.scalar_tensor_tensor(out=x[:], in0=x[:], scalar=gfin[:, kc:kc + 1], in1=rstd[:], op0=OP.mult,
                                                          op1=OP.mult), reads=[x.b, gfin.b, rstd.b], writes=[x.b])
                em.dma("sp", YT[kc * 128:(kc + 1) * 128, :], x[:], reads=[x.b])
            em.barrier()
        with contextlib.ExitStack() as st:
            yi = [sb("f_y%d" % i, [128, NKC, 128], F32, st) for i in range(2)]
            yo = [sb("f_o%d" % i, [128, D], F32, st) for i in range(2)]
            for tb in range(16):
                y = yi[tb % 2]
                o = yo[tb % 2]
                load(y, YT[:, tb * 128:(tb + 1) * 128].rearrange("(k p) t -> p k t", p=128), rd=[YT.b])
                for k4 in range(8):
                    pt = ps[k4 % 4]
                    em.deps(G, [y.b, ident_f.b], [pt.b])
                    for j in range(4):
                        ins = nc.tensor.transpose(pt[:, j * 128:(j + 1) * 128], y[:, k4 * 4 + j, :], ident_f[:, :])
                    em.done(G, ins, [y.b, ident_f.b], [pt.b])
                    if k4 % 2 == 0:
                        em.op(A, lambda e: e.activation(out=o[:, k4 * 512:(k4 + 1) * 512], in_=pt[:], func=AF.Copy), reads=[pt.b],
                              writes=[o.b])
                    else:
                        em.op(V, lambda e: e.tensor_copy(out=o[:, k4 * 512:(k4 + 1) * 512], in_=pt[:]), reads=[pt.b], writes=[o.b])
                em.dma("sp", out[tb * 128:(tb + 1) * 128, :], o[:], reads=[o.b], writes=[])
            em.barrier()
    if dbg:
        for name in dbg.get("dump", []):
            t, shape, dt = _scr[name]
            o_ = nc.dram_tensor("dbg_" + name, shape, dt, kind="ExternalOutput").ap()
            rows = shape[0]
            step = max(128, (rows // 8) // 128 * 128)
            for r0 in range(0, rows, step):
                r1 = min(rows, r0 + step)
                em.dma("sp", o_[r0:r1, :], t[r0:r1, :])
        em.barrier()


_NC_CACHE = {}


def _fm(v, nk):
    return np.ascontiguousarray(np.asarray(v, np.float32).reshape(nk, 128).T)


def make_in_maps(x, c, ctx, c_ctx, w_mod, b_mod, g_norm1, w_in, lb_logits, g_onorm, w_a, conv_w, conv_b, ln_g, ln_b, w_b, w_o,
           g_norm2, w_ffn_gate, w_ffn_up, w_ffn_down, g_final):
    f = lambda a: np.ascontiguousarray(np.asarray(a, dtype=np.float32))
    x, c, ctx, c_ctx = f(x), f(c), f(ctx), f(c_ctx)
    w_mod, b_mod = f(w_mod), f(b_mod)
    csT = np.ascontiguousarray(np.stack([c[0], c[1], c_ctx, np.zeros_like(c_ctx)], axis=-1).reshape(NKC, 128, 4).transpose(1, 0, 2))
    shared = {
        "csT": csT,
        "g1": np.ascontiguousarray(f(g_norm1).reshape(DEPTH, NKC, 128).transpose(2, 0, 1)),
        "g2": np.ascontiguousarray(f(g_norm2).reshape(DEPTH, NKC, 128).transpose(2, 0, 1)),
        "gfin": _fm(g_final, NKC),
        "w_in": f(w_in),
        "lbl": np.ascontiguousarray(f(lb_logits).reshape(DEPTH, 2, NH, 128).transpose(3, 0, 1, 2)),
        "gon": np.ascontiguousarray(f(g_onorm).reshape(DEPTH, NH, 128).transpose(2, 0, 1)),
        "w_a": f(w_a),
        "cw": np.ascontiguousarray(f(conv_w).reshape(DEPTH, 31, 16, 128).transpose(3, 0, 2, 1)),
        "cb": np.ascontiguousarray(f(conv_b).reshape(DEPTH, 16, 128).transpose(2, 0, 1)),
        "lng": np.ascontiguousarray(f(ln_g).reshape(DEPTH, 16, 128).transpose(2, 0, 1)),
        "lnb": np.ascontiguousarray(f(ln_b).reshape(DEPTH, 16, 128).transpose(2, 0, 1)),
        "w_b": f(w_b), "w_o": f(w_o), "w_g": f(w_ffn_gate), "w_u": f(w_ffn_up), "w_d": f(w_ffn_down),
    }
    in_maps = []
    for r in range(8):
        b, j = r // 4, r % 4
        fl = np.zeros((128, 80), np.float32)
        fl[:, b] = 1.0
        for r2 in range(8):
            b2, j2 = r2 // 4, r2 % 4
            same = float(b2 == b)
            fwp = float(b2 == b and j2 < j)
            bwp = float(b2 == b and j2 > j)
            fl[:, 8 + r2] = fwp
            fl[:, 16 + r2] = same
            fl[:, 24 + r2] = bwp
            fl[:, 32 + r2] = 1.0 - fwp
            fl[:, 40 + r2] = 1.0 - same
            fl[:, 48 + r2] = 1.0 - bwp
            fl[:, 56 + r2] = float(b2 == b and j2 == j - 1)
            fl[:, 64 + r2] = float(b2 == b and j2 == j + 1)
        g0 = NLAT * j
        pos = np.stack([(np.arange(g0, g0 + NLAT) // 64), (np.arange(g0, g0 + NLAT) % 64)]).astype(np.float32)
        m = dict(shared)
        m.update({
            "x_lat": np.ascontiguousarray(x[b, g0:g0 + NLAT]),
            "x_ctx": np.ascontiguousarray(ctx[b, NCTX * j:NCTX * (j + 1)]),
            "posv": pos,
            "flags": fl,
            "wmod": np.ascontiguousarray(w_mod[:, :, r * 3072:(r + 1) * 3072]),
            "bmod": np.ascontiguousarray(b_mod[:, r * 3072:(r + 1) * 3072].reshape(DEPTH, 24, 128).transpose(2, 0, 1)),
        })
        in_maps.append(m)
    return in_maps


def kernel(**inputs):
    in_maps = make_in_maps(**inputs)
    if "nc" not in _NC_CACHE:
        _NC_CACHE["nc"] = build()
    nc = _NC_CACHE["nc"]
    res = run_bass_kernel_spmd(nc, in_maps, core_ids=list(range(8)))
    outp = np.zeros((2, 8192, D), np.float32)
    for r in range(8):
        b, j = r // 4, r % 4
        outp[b, NLAT * j:NLAT * (j + 1)] = res.results[r]["out"]
    return outp
```
